# Optimizing a Trainium2 kernel written in Bass

```python
import math
import jax, jax.numpy as jnp
from jax import lax
import numpy as np

D_MODEL = 2048
BATCH = 1
SEQ = 16384
DEPTH = 1

CHUNK = 64
Q_BLOCK = 128
EPS = 1e-6
ROPE_THETA = 10000.0

M_HEADS = 4
M_DV = D_MODEL // 2 // M_HEADS
M_DQK = M_DV // 2
M_CONV = 4
GATE_CAP = 15.0
M_WIDTH = M_HEADS * M_DV

A_HEADS = 8
A_DH = D_MODEL // 2 // A_HEADS // 2
A_DV = 2 * A_DH
A_WIDTH = A_HEADS * A_DV

D_FF = -(-8 * D_MODEL // (3 * 256)) * 256

N_MQK = M_HEADS * M_DQK
IN_SIZES = (N_MQK, N_MQK, M_WIDTH, M_WIDTH, M_HEADS, M_HEADS,
            2 * A_HEADS * A_DH, 2 * A_HEADS * A_DH, A_WIDTH)
N_IN = sum(IN_SIZES)

kernel_name = "hybrid_mlstm_diffattn_block"


def rmsnorm(x, g):
    x32 = x.astype(jnp.float32)
    y = x32 * lax.rsqrt(jnp.mean(x32 * x32, axis=-1, keepdims=True) + EPS)
    return (y * g.astype(jnp.float32)).astype(x.dtype)


def split_proj(p):
    idx = np.cumsum(np.array(IN_SIZES))[:-1].tolist()
    return jnp.split(p, idx, axis=-1)


def rope(t, pos):
    half = t.shape[-1] // 2
    inv = ROPE_THETA ** (-jnp.arange(half, dtype=jnp.float32) / half)
    ang = pos.astype(jnp.float32)[:, None] * inv[None, :]
    cos = jnp.cos(ang).astype(t.dtype)
    sin = jnp.sin(ang).astype(t.dtype)
    t1, t2 = t[..., :half], t[..., half:]
    return jnp.concatenate([t1 * cos - t2 * sin, t2 * cos + t1 * sin], axis=-1)


def causal_conv(x, w, b):
    K = w.shape[0]
    S = x.shape[1]
    xp = jnp.pad(x, ((0, 0), (K - 1, 0), (0, 0)))
    y = b
    for j in range(K):
        y = y + xp[:, j:j + S] * w[j]
    return y


def mlstm_chunkwise(q, k, v, i_pre, f_pre):
    B, H, S, Dqk = q.shape
    Dv = v.shape[-1]
    NC = S // CHUNK
    q = q.astype(jnp.float32) * (Dqk ** -0.5)
    k = k.astype(jnp.float32)
    v = v.astype(jnp.float32)
    logf = jax.nn.log_sigmoid(f_pre.astype(jnp.float32))
    logi = i_pre.astype(jnp.float32)

    def to_chunks(t):
        return jnp.moveaxis(t.reshape((B, H, NC, CHUNK) + t.shape[3:]), 2, 0)

    qc, kc, vc, ic, fc = map(to_chunks, (q, k, v, logi, logf))
    causal = jnp.tril(jnp.ones((CHUNK, CHUNK), dtype=bool))

    def step(carry, xs):
        C, n, m = carry
        qb, kb, vb, ib, fb = xs
        b = jnp.cumsum(fb, axis=-1)
        logD = jnp.where(causal, b[..., :, None] - b[..., None, :] + ib[..., None, :], -jnp.inf)
        inter = b + m[..., None]
        m_t = jnp.maximum(inter, jnp.max(logD, axis=-1))
        Dw = jnp.exp(logD - m_t[..., None])
        inter_w = jnp.exp(inter - m_t)
        s = jnp.einsum('bhtd,bhsd->bhts', qb, kb) * Dw
        num = inter_w[..., None] * jnp.einsum('bhtd,bhde->bhte', qb, C) \
            + jnp.einsum('bhts,bhse->bhte', s, vb)
        den = inter_w * jnp.einsum('bhtd,bhd->bht', qb, n) + jnp.sum(s, axis=-1)
        h = num / jnp.maximum(jnp.abs(den), jnp.exp(-m_t))[..., None]
        bL = b[..., -1]
        w_log = bL[..., None] - b + ib
        m_new = jnp.maximum(bL + m, jnp.max(w_log, axis=-1))
        decay = jnp.exp(bL + m - m_new)
        kw = kb * jnp.exp(w_log - m_new[..., None])[..., None]
        C_new = decay[..., None, None] * C + jnp.einsum('bhsd,bhse->bhde', kw, vb)
        n_new = decay[..., None] * n + jnp.sum(kw, axis=2)
        return (C_new, n_new, m_new), h

    init = (jnp.zeros((B, H, Dqk, Dv), jnp.float32),
            jnp.zeros((B, H, Dqk), jnp.float32),
            jnp.zeros((B, H), jnp.float32))
    _, hc = lax.scan(step, init, (qc, kc, vc, ic, fc))
    return jnp.moveaxis(hc, 0, 2).reshape(B, H, S, Dv)


def diff_attention(q, k, v, lam):
    B, H, _, S, Dh = q.shape
    NB = S // Q_BLOCK
    scale = Dh ** -0.5
    key_chunk = jnp.arange(S) // CHUNK
    qb = jnp.moveaxis(q.reshape(B, H, 2, NB, Q_BLOCK, Dh), 3, 0)
    v32 = v.astype(jnp.float32)

    def block(args):
        qblk, start = args
        s = jnp.einsum('bhcqd,bhckd->bhcqk', qblk, k,
                       preferred_element_type=jnp.float32) * scale
        q_chunk = (start + jnp.arange(Q_BLOCK)) // CHUNK
        mask = key_chunk[None, :] <= q_chunk[:, None]
        p = jax.nn.softmax(jnp.where(mask, s, -jnp.inf), axis=-1)
        a = p[:, :, 0] - lam * p[:, :, 1]
        return jnp.einsum('bhqk,bhkv->bhqv', a, v32)

    out = lax.map(block, (qb, jnp.arange(NB) * Q_BLOCK))
    return jnp.moveaxis(out, 0, 2).reshape(B, H, S, v.shape[-1])


def setup_inputs(seed: int = 0) -> dict:
    key = jax.random.key(seed)
    ks = jax.random.split(key, 20)
    f32 = jnp.float32

    def nrm(k, shape, s):
        return s * jax.random.normal(k, shape, f32)

    return {
        "x": nrm(ks[0], (BATCH, SEQ, D_MODEL), 1.0),
        "norm1_g": 1.0 + nrm(ks[1], (DEPTH, D_MODEL), 0.02),
        "w_in": nrm(ks[2], (DEPTH, D_MODEL, N_IN), D_MODEL ** -0.5),
        "conv_w": nrm(ks[3], (DEPTH, M_CONV, 2 * N_MQK), M_CONV ** -0.5),
        "conv_b": nrm(ks[4], (DEPTH, 2 * N_MQK), 0.02),
        "b_igate": nrm(ks[5], (DEPTH, M_HEADS), 0.1),
        "b_fgate": jnp.linspace(3.0, 6.0, M_HEADS, dtype=f32)[None, :] + nrm(ks[6], (DEPTH, M_HEADS), 0.1),
        "mnorm_g": 1.0 + nrm(ks[7], (DEPTH, M_HEADS, M_DV), 0.02),
        "lambda_q1": nrm(ks[8], (DEPTH, A_DH), 0.1),
        "lambda_k1": nrm(ks[9], (DEPTH, A_DH), 0.1),
        "lambda_q2": nrm(ks[10], (DEPTH, A_DH), 0.1),
        "lambda_k2": nrm(ks[11], (DEPTH, A_DH), 0.1),
        "subln_g": 1.0 + nrm(ks[12], (DEPTH, A_DV), 0.02),
        "w_out": nrm(ks[13], (DEPTH, M_WIDTH + A_WIDTH, D_MODEL), (M_WIDTH + A_WIDTH) ** -0.5),
        "norm2_g": 1.0 + nrm(ks[14], (DEPTH, D_MODEL), 0.02),
        "w_gate": nrm(ks[15], (DEPTH, D_MODEL, D_FF), D_MODEL ** -0.5),
        "w_up": nrm(ks[16], (DEPTH, D_MODEL, D_FF), D_MODEL ** -0.5),
        "w_down": nrm(ks[17], (DEPTH, D_FF, D_MODEL), D_FF ** -0.5),
        "final_g": 1.0 + nrm(ks[18], (D_MODEL,), 0.02),
    }


def reference(x, norm1_g, w_in, conv_w, conv_b, b_igate, b_fgate, mnorm_g,
              lambda_q1, lambda_k1, lambda_q2, lambda_k2, subln_g, w_out,
              norm2_g, w_gate, w_up, w_down, final_g):
    B, S, _ = x.shape
    pos = jnp.arange(S)
    for l in range(DEPTH):
        h = rmsnorm(x, norm1_g[l])
        proj = h @ w_in[l]
        mq, mk, mv, mo, mi, mf, aq, ak, av = split_proj(proj)

        qk = jax.nn.silu(causal_conv(jnp.concatenate([mq, mk], axis=-1), conv_w[l], conv_b[l]))
        mq, mk = qk[..., :N_MQK], qk[..., N_MQK:]
        mq = mq.reshape(B, S, M_HEADS, M_DQK).transpose(0, 2, 1, 3)
        mk = mk.reshape(B, S, M_HEADS, M_DQK).transpose(0, 2, 1, 3)
        mv_h = mv.reshape(B, S, M_HEADS, M_DV).transpose(0, 2, 1, 3)
        i_pre = GATE_CAP * jnp.tanh((mi + b_igate[l]).astype(jnp.float32) / GATE_CAP)
        f_pre = GATE_CAP * jnp.tanh((mf + b_fgate[l]).astype(jnp.float32) / GATE_CAP)
        hm = mlstm_chunkwise(mq, mk, mv_h, i_pre.transpose(0, 2, 1), f_pre.transpose(0, 2, 1))
        hm = rmsnorm(hm, mnorm_g[l][:, None, :]).astype(x.dtype)
        hm = hm.transpose(0, 2, 1, 3).reshape(B, S, M_WIDTH) * jax.nn.sigmoid(mo)

        aq_h = rope(aq.reshape(B, S, A_HEADS, 2, A_DH).transpose(0, 2, 3, 1, 4), pos)
        ak_h = rope(ak.reshape(B, S, A_HEADS, 2, A_DH).transpose(0, 2, 3, 1, 4), pos)
        av_h = av.reshape(B, S, A_HEADS, A_DV).transpose(0, 2, 1, 3)
        lam_init = 0.8 - 0.6 * math.exp(-0.3 * l)
        lam = (jnp.exp(jnp.sum(lambda_q1[l].astype(jnp.float32) * lambda_k1[l].astype(jnp.float32)))
               - jnp.exp(jnp.sum(lambda_q2[l].astype(jnp.float32) * lambda_k2[l].astype(jnp.float32)))
               + lam_init)
        ha = diff_attention(aq_h, ak_h, av_h, lam)
        ha = rmsnorm(ha, subln_g[l]) * (1.0 - lam_init)
        ha = ha.transpose(0, 2, 1, 3).reshape(B, S, A_WIDTH).astype(x.dtype)

        x = x + jnp.concatenate([hm, ha], axis=-1) @ w_out[l]

        h2 = rmsnorm(x, norm2_g[l])
        x = x + (jax.nn.silu(h2 @ w_gate[l]) * (h2 @ w_up[l])) @ w_down[l]
    return rmsnorm(x, final_g)
```

```python
import math
from contextlib import ExitStack

import numpy as np
import concourse.bass as bass
import concourse.mybir as mybir
from concourse.bass_utils import run_bass_kernel_spmd

F32 = mybir.dt.float32
BF16 = mybir.dt.bfloat16
AF = mybir.ActivationFunctionType
ALU = mybir.AluOpType

S = 16384
D = 2048
DFF = 5632
TS_ = 512
NT = S // TS_
NCORE = 8
TOKB = S // NCORE
EPS = 1e-6
NTOK_A = 770
FG = 256
NFG = DFF // FG

N_TILES_A = NT
EPI_N = [99]


class Prog:
    def __init__(self, nc, es):
        self.nc = nc
        self.es = es
        self.q = {k: [] for k in ("pe", "act", "dve", "pool", "sp")}
        self.sems = {}
        self.cnt = {}
        self.waited = {}
        self.bw = {}
        self.br = {}
        self.check = True
        self.cap = None
        self.base = {}

    def sem(self, name):
        if name not in self.sems:
            self.sems[name] = self.es.enter_context(self.nc.semaphore(name))
            self.cnt[name] = 0
        return self.sems[name]

    def _wait(self, eng, ev):
        if ev is None:
            return
        name, val = ev
        if eng == "pe" and name == "pe":
            return
        if self.waited.get((eng, name), 0) >= val:
            return
        self.waited[(eng, name)] = val
        self.q[eng].append(("w", name, val))

    def _deps(self, eng, reads, writes):
        for k in reads:
            self._wait(eng, self.bw.get(k))
        for k in writes:
            self._wait(eng, self.bw.get(k))
            for ev in list(self.br.get(k, {}).items()):
                self._wait(eng, ev)

    def _record(self, ev, reads, writes):
        for k in reads:
            d = self.br.setdefault(k, {})
            d[ev[0]] = max(d.get(ev[0], 0), ev[1])
        for k in writes:
            self.bw[k] = ev
            self.br[k] = {}

    def op(self, eng, fn, reads=(), writes=()):
        return self.group(eng, [fn], reads, writes)

    @staticmethod
    def _norm(reads, writes):
        reads = list(reads)
        writes = list(writes)
        for k in list(reads):
            if len(k) == 2 and k[0] == "b" and k[1].isdigit():
                reads.remove(k)
                if k not in writes:
                    writes.append(k)
        return reads, writes

    def begin(self):
        self.cap = []

    def end(self):
        lst = self.cap
        self.cap = None
        return lst

    def replay(self, lists):
        lists = [l for l in lists if l]
        idx = [0] * len(lists)
        while True:
            best = None
            for i, l in enumerate(lists):
                if idx[i] < len(l):
                    frac = (idx[i] + 1) / len(l)
                    if best is None or frac < best[0]:
                        best = (frac, i)
            if best is None:
                break
            i = best[1]
            it = lists[i][idx[i]]
            idx[i] += 1
            if it[0] == "g":
                self.group(*it[1:])
            else:
                self.dma(*it[1:])

    def group(self, eng, fns, reads=(), writes=()):
        if self.cap is not None:
            self.cap.append(("g", eng, fns, reads, writes))
            return None
        reads, writes = self._norm(reads, writes)
        self._deps(eng, reads, writes)
        self.sem(eng)
        self.cnt[eng] += 1
        ev = (eng, self.cnt[eng])
        for f in fns[:-1]:
            self.q[eng].append(("n", f))
        self.q[eng].append(("o", fns[-1], eng, 1))
        self._record(ev, reads, writes)
        return ev

    def dma(self, qeng, fn, semname, reads=(), writes=(), inc=16):
        if self.cap is not None:
            self.cap.append(("d", qeng, fn, semname, reads, writes, inc))
            return None
        reads, writes = self._norm(reads, writes)
        self._deps(qeng, reads, writes)
        self.sem(semname)
        self.cnt[semname] += inc
        ev = (semname, self.cnt[semname])
        self.q[qeng].append(("o", fn, semname, inc))
        self._record(ev, reads, writes)
        return ev

    def barrier(self):
        for eng in self.q:
            for name in list(self.sems):
                if self.cnt[name] > 0:
                    self._wait(eng, (name, self.cnt[name]))
        self.bw = {}
        self.br = {}

    def simulate(self):
        pos = {k: 0 for k in self.q}
        val = {k: self.base.get(k, 0) for k in self.sems}
        progress = True
        while progress:
            progress = False
            for k, items in self.q.items():
                while pos[k] < len(items):
                    it = items[pos[k]]
                    if it[0] == "w":
                        if val[it[1]] < it[2]:
                            break
                    elif it[0] == "o":
                        val[it[2]] += it[3]
                    pos[k] += 1
                    progress = True
        stuck = {k: (pos[k], len(v), v[pos[k]][:3]) for k, v in self.q.items() if pos[k] < len(v)}
        self.base = val
        return stuck

    def emit(self):
        nc = self.nc
        if self.check:
            stuck = self.simulate()
            assert not stuck, stuck
        sems = self.sems

        def runner(name):
            items = self.q[name]

            def f(e):
                for it in items:
                    if it[0] == "w":
                        e.wait_ge(sems[it[1]], it[2])
                    elif it[0] == "n":
                        it[1](e)
                    else:
                        it[1](e).then_inc(sems[it[2]], it[3])
            return f

        with nc.Block() as block:
            block.tensor(runner("pe"))
            block.scalar(runner("act"))
            block.vector(runner("dve"))
            block.gpsimd(runner("pool"))
            block.sync(runner("sp"))
        for k in self.q:
            self.q[k] = []


def build_program(n_tiles_a=N_TILES_A, do_phase_b=True, debug=False, skip=()):
    nc = bass.Bass("TRN2", target_bir_lowering=False)
    S_A = n_tiles_a * TS_

    def din(name, shape, dt=F32):
        return nc.dram_tensor(name, list(shape), dt, kind="ExternalInput").ap()

    xT = din("xT", [D, S_A])
    wtok = din("wtok", [D, NTOK_A])
    wf = din("wf", [D, 256])
    g1_d = din("g1", [128, 16])
    convw_d = din("convw", [128, 8])
    convb_d = din("convb", [128, 2])
    bg_d = din("bg", [128, 2])
    mg_d = din("mg", [128, 128])
    lamv_d = din("lamv", [128, 4, 64])
    sg_d = din("sg", [128, 1])
    cos_d = din("cost", [S_A, 32])
    sin_d = din("sint", [S_A, 32])
    ident_d = din("ident", [128, 128])
    t2_d = din("t2", [128, 128])
    sel_d = din("sel127", [128, 128])
    mst_d = din("maskst", [128, 128])
    mka_d = din("maska", [128, 128])
    if do_phase_b:
        xTb = din("xTb", [D, TOKB])
        wo_d = din("wo", [D, D])
        g2_d = din("g2", [128, 16])
        wg_d = din("wgate", [D, DFF])
        wu_d = din("wup", [D, DFF])
        wd_d = din("wdown", [DFF, D])
        gf_d = din("gf", [128, 16])
        outT = nc.dram_tensor("outT", [D, TOKB], F32, kind="ExternalOutput").ap()
        agout = nc.dram_tensor("agout", [NCORE * 256, S], BF16, kind="Internal").ap()
    agin = nc.dram_tensor("agin", [256, S_A], BF16, kind="ExternalOutput" if debug else "Internal").ap()

    with ExitStack() as es:
        P = Prog(nc, es)

        def sb(stack, name, shape, dt):
            return stack.enter_context(nc.sbuf_tensor(name, list(shape), dt))

        ps = [es.enter_context(nc.psum_tensor(f"ps{b}", [128, 512], F32)) for b in range(7)]
        psT = es.enter_context(nc.psum_tensor("psT", [128, 1024], BF16))

        def ACT(out, in_, func, reads, writes, bias=None, scale=None, accum=None):
            kw = {}
            if bias is not None:
                kw["bias"] = bias
            if scale is not None:
                kw["scale"] = scale
            if accum is not None:
                kw["accum_out"] = accum
            return P.op("act", lambda e: e.activation(out, in_, func, **kw), reads, writes)

        def TT(eng, out, in0, in1, op, reads, writes):
            return P.op(eng, lambda e: e.tensor_tensor(out, in0, in1, op), reads, writes)

        def TSC(eng, out, in0, s1, s2, op0, op1, reads, writes):
            return P.op(eng, lambda e: e.tensor_scalar(out, in0, s1, s2, op0, op1), reads, writes)

        def STT(out, in0, scalar, in1, op0, op1, reads, writes):
            return P.op("dve", lambda e: e.scalar_tensor_tensor(out, in0, scalar, in1, op0, op1), reads, writes)

        def RECIP(out, in_, reads, writes):
            return P.op("dve", lambda e: e.reciprocal(out, in_), reads, writes)

        def COPY(eng, out, in_, reads, writes):
            if eng == "act":
                return P.op("act", lambda e: e.copy(out, in_), reads, writes)
            return P.op(eng, lambda e: e.tensor_copy(out, in_), reads, writes)

        def MEMSET(eng, ap, val, writes):
            return P.op(eng, lambda e: e.memset(ap, val), (), writes)

        def MMG(mms, reads, writes):
            fns = []
            for mm in mms:
                o, l, r, st, sp_ = mm[:5]
                sk = len(mm) > 5 and mm[5]
                fns.append(lambda e, o=o, l=l, r=r, st=st, sp_=sp_, sk=sk: e.matmul(o, l, r, start=st, stop=sp_,
                                                                                  skip_group_check=sk))
            return P.group("pe", fns, reads, writes)

        def TRP(out, in_, ident, reads, writes):
            return P.op("pe", lambda e: e.transpose(out, in_, ident), reads, writes)

        def DMA(q, out, in_, semname, reads, writes):
            return P.dma(q, lambda e: e.dma_start(out=out, in_=in_), semname, reads, writes)

        epsc = sb(es, "epsc", [128, 1], F32)
        onec = sb(es, "onec", [128, 1], F32)
        ones_bf = sb(es, "ones_bf", [128, 128], BF16)
        MEMSET("pool", epsc[:, :], EPS, ["epsc"])
        MEMSET("pool", onec[:, :], 1.0, ["onec"])
        MEMSET("pool", ones_bf[:, :], 1.0, ["ones_bf"])
        qsc = sb(es, "qsc", [128, 1], F32)
        MEMSET("pool", qsc[:, :], 128.0 ** -0.5, ["qsc"])

        def rms_rstd(src_key, src3, sq3, sq_key, bank, bank_key, rtmp, rstd, n_feat):
            ACT(sq3, src3, AF.Square, [src_key], [sq_key])
            MMG([(bank[:, :], ones_bf[:, :], sq3[:, c, :], c == 0, c == 15) for c in range(16)],
                [sq_key, "ones_bf"], [bank_key])
            ACT(rtmp, bank[:, :], AF.Sqrt, [bank_key, "epsc"], ["rtmp"], bias=epsc[:, 0:1], scale=1.0 / n_feat)
            RECIP(rstd, rtmp, ["rtmp"], ["rstd"])

        with ExitStack() as ea:
            g1 = sb(ea, "g1s", [128, 16], F32)
            convw = sb(ea, "convw_s", [128, 8], F32)
            convb = sb(ea, "convb_s", [128, 2], F32)
            bg = sb(ea, "bg_s", [128, 2], F32)
            mg = sb(ea, "mg_s", [128, 128], F32)
            lamv = sb(ea, "lamv_s", [128, 4, 64], F32)
            lamp = sb(ea, "lamp", [128, 2, 64], F32)
            lams = sb(ea, "lams", [128, 4], F32)
            ident = sb(ea, "ident_s", [128, 128], BF16)
            T2 = sb(ea, "T2_s", [128, 128], F32)
            Sel = sb(ea, "Sel_s", [128, 128], F32)
            maskST = sb(ea, "maskST_s", [128, 128], F32)
            maskA = sb(ea, "maskA_s", [128, 128], BF16)
            ones_f = sb(ea, "ones_f", [128, 128], F32)
            sgcol = sb(ea, "sgcol", [128, 1], F32)
            qT_d = nc.dram_tensor("qT_d", [128, S_A], BF16, kind="Internal").ap()
            kT_d = nc.dram_tensor("kT_d", [128, S_A], BF16, kind="Internal").ap()
            v_d = nc.dram_tensor("v_d", [128, S_A], BF16, kind="Internal").ap()

            DMA("sp", g1[:, :], g1_d[:, :], "cst", [], ["g1"])
            DMA("sp", convw[:, :], convw_d[:, :], "cst", [], ["convw"])
            DMA("sp", convb[:, :], convb_d[:, :], "cst", [], ["convb"])
            DMA("sp", bg[:, :], bg_d[:, :], "cst", [], ["bg"])
            DMA("sp", mg[:, :], mg_d[:, :], "cst", [], ["mg"])
            DMA("sp", lamv[:, :, :], lamv_d[:, :, :], "cst", [], ["lamv"])
            DMA("sp", sgcol[:, :], sg_d[:, :], "cst", [], ["sgraw"])
            DMA("sp", T2[:, :], t2_d[:, :], "cst", [], ["T2c"])
            DMA("sp", Sel[:, :], sel_d[:, :], "cst", [], ["Sel"])
            DMA("sp", maskST[:, :], mst_d[:, :], "cst", [], ["maskST"])
            cst_final = ("cst", P.cnt["cst"])
            for k in ["g1", "convw", "convb", "bg", "mg", "lamv", "sgraw", "T2c", "Sel", "maskST"]:
                P.bw[k] = cst_final
            DMA("pool", ident[:, :], ident_d[:, :], "cstp", [], ["ident"])
            DMA("pool", maskA[:, :], mka_d[:, :], "cstp", [], ["maskA"])

            TT("dve", lamp[:, :, :], lamv[:, 0:2, :], lamv[:, 2:4, :], ALU.mult, ["lamv"], ["lamp"])
            P.op("dve", lambda e: e.tensor_reduce(lams[:, 0:2], lamp[:, :, :], mybir.AxisListType.X, ALU.add),
                 ["lamp"], ["lams01"])
            ACT(lams[:, 0:2], lams[:, 0:2], AF.Exp, ["lams01"], ["lams01"])
            TT("dve", lams[:, 2:3], lams[:, 1:2], lams[:, 0:1], ALU.subtract, ["lams01"], ["lams2"])
            TSC("dve", lams[:, 3:4], lams[:, 2:3], -0.2, None, ALU.add, ALU.bypass, ["lams2"], ["nlam"])
            TSC("dve", sgcol[:, :], sgcol[:, :], 0.8, None, ALU.mult, ALU.bypass, ["sgraw"], ["sgcol"])
            MEMSET("pool", ones_f[:, :], 1.0, ["ones_f"])

            xT_v = xT.rearrange("(c p) t -> p c t", p=128)
            cos_v = cos_d.rearrange("(b p) f -> p b f", p=128)
            sin_v = sin_d.rearrange("(b p) f -> p b f", p=128)
            HT = TS_ // 2

            with ExitStack() as e1:
                xs2 = [sb(e1, f"xs{i}", [128, 16, HT], F32) for i in range(2)]
                KTst = sb(e1, "KTst", [128, TS_], BF16)
                Vst = sb(e1, "Vst", [128, 4, 128], BF16)
                xn = [sb(e1, f"xn{i}", [128, 16, TS_], BF16) for i in range(2)]
                rtmp = sb(e1, "rtmp", [128, HT], F32)
                rstd = sb(e1, "rstd", [128, HT], F32)
                Wtok = sb(e1, "Wtok", [128, 16, NTOK_A], BF16)
                Wf = sb(e1, "Wf", [128, 16, 256], BF16)
                QT = sb(e1, "QT", [128, TS_], BF16)
                cosb = [sb(e1, f"cosb{i}", [128, 4, 32], F32) for i in range(2)]
                sinb = [sb(e1, f"sinb{i}", [128, 4, 32], F32) for i in range(2)]
                ropeA = [sb(e1, f"ropeA{i}", [128, 256], F32) for i in range(2)]
                ropeB = [sb(e1, f"ropeB{i}", [128, 256], F32) for i in range(2)]
                ropeR = [sb(e1, f"ropeR{i}", [128, 256], BF16) for i in range(2)]
                qkv = [sb(e1, f"qkv{i}", [128, 384], F32) for i in range(2)]
                g2s = sb(e1, "g2evac", [128, 4, 386], F32)
                preq = sb(e1, "preq", [128, TS_ + 3], F32)
                prek = sb(e1, "prek", [128, TS_ + 3], F32)
                cacc = sb(e1, "cacc", [128, TS_], F32)
                csig = sb(e1, "csig", [128, TS_], F32)
                graw = sb(e1, "graw", [128, 4, 2], F32)
                gth = sb(e1, "gth", [128, 4, 2], F32)
                gexp = sb(e1, "gexp", [128, 4], F32)
                lsb = sb(e1, "lsb", [128, 4], F32)
                cumsb = sb(e1, "cumsb", [128, 4], F32)
                garg = sb(e1, "garg", [128, 4], F32)
                gu = sb(e1, "gu", [128, 4], F32)
                mqT = [sb(e1, f"mqT{i}", [128, TS_], BF16) for i in range(2)]
                mkT = [sb(e1, f"mkT{i}", [128, TS_], BF16) for i in range(2)]
                kk = [sb(e1, f"kk{i}", [128, 4, 128], BF16) for i in range(2)]
                vt = [sb(e1, f"vt{i}", [128, 4, 257], BF16) for i in range(2)]
                og = [sb(e1, f"og{i}", [128, 4, 128], F32) for i in range(2)]
                gfl = [sb(e1, f"gfl{i}", [128, 4], F32) for i in range(2)]
                decbuf = [sb(e1, f"decbuf{i}", [128, 5], F32) for i in range(2)]
                X = sb(e1, "X", [128, 257], F32)
                Cb = [sb(e1, f"Cb{i}", [128, 257], BF16) for i in range(2)]
                Sm = [sb(e1, f"Sm{i}", [128, 128], BF16) for i in range(2)]
                sm1 = [sb(e1, f"sm1_{i}", [128, 8], F32) for i in range(2)]
                junk = [sb(e1, f"junk{i}", [128, 256], F32) for i in range(2)]
                gg = [sb(e1, f"gg{i}", [128, 128], F32) for i in range(2)]
                ybf = [sb(e1, f"ybf{i}", [128, 128], BF16) for i in range(2)]
                zTm = sb(e1, "zTm", [128, TS_], BF16)

                DMA("pool", Wtok[:, :, :], wtok.rearrange("(c p) n -> p c n", p=128), "cstp", [], ["Wtok"])
                DMA("pool", Wf[:, :, :], wf.rearrange("(c p) n -> p c n", p=128), "cstp", [], ["Wf"])
                cstp_final = ("cstp", P.cnt["cstp"])
                for k in ["ident", "maskA", "Wtok", "Wf"]:
                    P.bw[k] = cstp_final

                MEMSET("pool", X[:, :], 0.0, ["X"])
                MEMSET("pool", decbuf[0][:, :], 1.0, ["decbuf0"])
                MEMSET("pool", decbuf[1][:, :], 1.0, ["decbuf1"])
                MEMSET("pool", preq[:, 0:3], 0.0, ["preq"])
                MEMSET("pool", prek[:, 0:3], 0.0, ["prek"])

                def stage_F(g):
                    par = g % 2
                    DMA("sp", cosb[par][:, :, :], cos_v[:, 4 * g:4 * g + 4, :], "ropec%d" % par, [], ["cosb%d" % par])
                    DMA("sp", sinb[par][:, :, :], sin_v[:, 4 * g:4 * g + 4, :], "ropes%d" % par, [], ["sinb%d" % par])
                    for h in range(2):
                        hs = slice(g * TS_ + h * HT, g * TS_ + (h + 1) * HT)
                        hl = slice(h * HT, (h + 1) * HT)
                        xkey = "xn%dh%d" % (par, h)
                        xs = xs2[h]
                        xsk = "xs%d" % h
                        DMA("sp", xs[:, :, :], xT_v[:, :, hs], xsk, [], [xsk])
                        sq3 = xn[par][:, :, hl]
                        ACT(sq3, xs[:, :, :], AF.Square, [xsk], [xkey])
                        MMG([(ps[0][:, 0:HT], ones_bf[:, :], sq3[:, c, :], c == 0, c == 15) for c in range(16)],
                            [xkey, "ones_bf"], ["b0"])
                        ACT(rtmp[:, :], ps[0][:, 0:HT], AF.Sqrt, ["b0", "epsc"], ["rtmp"], bias=epsc[:, 0:1], scale=1.0 / D)
                        RECIP(rstd[:, :], rtmp[:, :], ["rtmp"], ["rstd"])
                        for c in range(16):
                            STT(xn[par][:, c, hl], xs[:, c, :], g1[:, c:c + 1], rstd[:, :], ALU.mult, ALU.mult,
                                [xsk, "g1", "rstd"], [xkey])

                def stage_P(g):
                    par = g % 2
                    pp = 1 - par
                    tsl = slice(g * TS_, (g + 1) * TS_)
                    xk = ["xn%dh0" % par, "xn%dh1" % par]
                    xnp = xn[par]
                    for j in range(4):
                        blk = 4 * g + j
                        js = slice(128 * j, 128 * j + 128)
                        xkey = "xn%dh%d" % (par, j // 2)
                        MMG([(ps[1][:, 0:384], xnp[:, c, js], Wtok[:, c, 0:384], c == 0, c == 15) for c in range(16)],
                            [xkey, "Wtok"], ["b1"])
                        MMG([(ps[2][:, 0:386], xnp[:, c, js], Wtok[:, c, 384:770], c == 0, c == 15) for c in range(16)],
                            [xkey, "Wtok"], ["b2"])
                        jp = j % 2
                        qk_ = "qkv%d" % jp
                        COPY("act", qkv[jp][:, :], ps[1][:, 0:384], ["b1"], [qk_])
                        COPY("act", g2s[:, j, :], ps[2][:, 0:386], ["b2"], ["g2s%d" % j])
                        COPY("pool", Vst[:, j, :], qkv[jp][:, 256:384], [qk_], ["Vst"])
                        cb = cosb[par][:, j, :].unsqueeze(1).broadcast_to([128, 8, 32])
                        sbb = sinb[par][:, j, :].unsqueeze(1).broadcast_to([128, 8, 32])
                        qk3 = qkv[jp][:, 0:256].rearrange("p (a f) -> p a f", f=32)
                        A3 = ropeA[jp][:, :].rearrange("p (a f) -> p a f", f=32)
                        B3 = ropeB[jp][:, :].rearrange("p (a f) -> p a f", f=32)
                        TT("dve", A3, qk3, cb, ALU.mult, [qk_, "cosb%d" % par], ["ropeA%d" % jp])
                        TT("dve", B3, qk3, sbb, ALU.mult, [qk_, "sinb%d" % par], ["ropeB%d" % jp])
                        A4 = ropeA[jp][:, :].rearrange("p (a h f) -> p a h f", h=2, f=32)
                        B4 = ropeB[jp][:, :].rearrange("p (a h f) -> p a h f", h=2, f=32)
                        R4 = ropeR[jp][:, :].rearrange("p (a h f) -> p a h f", h=2, f=32)
                        TT("pool", R4[:, :, 0, :], A4[:, :, 0, :], B4[:, :, 1, :], ALU.subtract,
                           ["ropeA%d" % jp, "ropeB%d" % jp], ["ropeR0_%d" % jp])
                        TT("pool", R4[:, :, 1, :], A4[:, :, 1, :], B4[:, :, 0, :], ALU.add,
                           ["ropeA%d" % jp, "ropeB%d" % jp], ["ropeR1_%d" % jp])
                        rk = ["ropeR0_%d" % jp, "ropeR1_%d" % jp, "ident"]
                        TRP(psT[:, 0:128], ropeR[jp][:, 0:128], ident[:, :], rk, ["b7"])
                        COPY("act", QT[:, js], psT[:, 0:128], ["b7"], ["QT"])
                        TRP(psT[:, 128:256], ropeR[jp][:, 128:256], ident[:, :], rk, ["b7"])
                        COPY("act", KTst[:, js], psT[:, 128:256], ["b7"], ["KTst"])
                        ACT(og[par][:, j, :], g2s[:, j, 256:384], AF.Sigmoid, ["g2s%d" % j], ["og%d" % par])
                        TT("dve", graw[:, j, :], g2s[:, j, 384:386], bg[:, :], ALU.add, ["g2s%d" % j, "bg"], ["graw"])
                    DMA("sp", qT_d[:, tsl], QT[:, :], "qtout", ["QT"], ["qTd%d" % g])
                    DMA("sp", kT_d[:, tsl], KTst[:, :], "ktout", ["KTst"], ["kTd%d" % g])
                    DMA("sp", v_d[:, tsl], Vst[:, :, :].rearrange("p a b -> p (a b)"), "vout", ["Vst"], ["vd%d" % g])
                    for n_, (pre, dst, scl, key) in enumerate([(preq, mqT[par], qsc[:, 0:1], "mqT%d" % par),
                                                                (prek, mkT[par], onec[:, 0:1], "mkT%d" % par)]):
                        pkey = "preq" if n_ == 0 else "prek"
                        MMG([(ps[3][:, :], Wf[:, c, 128 * n_:128 * n_ + 128], xnp[:, c, :], c == 0, c == 15) for c in range(16)],
                            xk + ["Wf"], ["b3"])
                        COPY("act", pre[:, 3:TS_ + 3], ps[3][:, :], ["b3"], [pkey])
                        TSC("dve", cacc[:, :], pre[:, 3:TS_ + 3], convw[:, 4 * n_ + 3:4 * n_ + 4], convb[:, n_:n_ + 1],
                            ALU.mult, ALU.add, [pkey, "convw", "convb"], ["cacc"])
                        for tap in (2, 1, 0):
                            STT(cacc[:, :], pre[:, tap:TS_ + tap], convw[:, 4 * n_ + tap:4 * n_ + tap + 1], cacc[:, :],
                                ALU.mult, ALU.add, [pkey, "convw", "cacc"], ["cacc"])
                        COPY("pool", pre[:, 0:3], pre[:, TS_:TS_ + 3], [pkey], [pkey])
                        ACT(csig[:, :], cacc[:, :], AF.Sigmoid, ["cacc"], ["csig"])
                        STT(dst[:, :], cacc[:, :], scl, csig[:, :], ALU.mult, ALU.mult, ["cacc", "csig", "qsc", "onec"], [key])
                    for j in range(4):
                        js = slice(128 * j, 128 * j + 128)
                        TRP(psT[:, 256:384], mkT[par][:, js], ident[:, :], ["mkT%d" % par, "ident"], ["b7"])
                        COPY("act", kk[par][:, j, :], psT[:, 256:384], ["b7"], ["kk%d" % par])
                    ACT(gth[:, :, :], graw[:, :, :], AF.Tanh, ["graw"], ["gth"], scale=1.0 / 15.0)
                    ACT(gexp[:, :], gth[:, :, 1], AF.Exp, ["gth"], ["gexp"], scale=-15.0)
                    ACT(lsb[:, :], gexp[:, :], AF.Ln, ["gexp", "onec"], ["lsb"], bias=onec[:, 0:1])
                    MMG([(ps[3][:, 0:4], T2[:, :], lsb[:, :], True, True)], ["T2c", "lsb"], ["b3"])
                    COPY("dve", cumsb[:, :], ps[3][:, 0:4], ["b3"], ["cumsb"])
                    STT(garg[:, :], gth[:, :, 0], 15.0, cumsb[:, :], ALU.mult, ALU.add, ["gth", "cumsb"], ["garg"])
                    ACT(gu[:, :], garg[:, :], AF.Exp, ["garg"], ["gu"])
                    ACT(gfl[par][:, :], cumsb[:, :], AF.Exp, ["cumsb"], ["gfl%d" % par])
                    MMG([(ps[3][:, 0:4], Sel[:, :], cumsb[:, :], True, True)], ["Sel", "cumsb"], ["b3"])
                    COPY("dve", decbuf[par][:, 0:1], decbuf[pp][:, 4:5], ["decbuf%d" % pp], ["decbuf%d" % par])
                    ACT(decbuf[par][:, 1:5], ps[3][:, 0:4], AF.Exp, ["b3"], ["decbuf%d" % par], scale=-1.0)
                    TT("dve", vt[par][:, :, 0:256], g2s[:, :, 0:256], gu[:, :].unsqueeze(2).broadcast_to([128, 4, 256]),
                       ALU.mult, ["g2s0", "g2s1", "g2s2", "g2s3", "gu"], ["vt%d" % par])
                    COPY("pool", vt[par][:, :, 256], gu[:, :], ["gu", "vt%d" % par], ["vt%d" % par])

                def stage_M(g):
                    par = g % 2
                    tsl = slice(g * TS_, (g + 1) * TS_)
                    dk = "decbuf%d" % par
                    for j in range(4):
                        js = slice(128 * j, 128 * j + 128)
                        jp = j % 2
                        pN = ps[5 + jp]
                        bN = "b%d" % (5 + jp)
                        s1 = sm1[jp]
                        sk = "sm%d_" % jp
                        MMG([(ps[4][:, 0:128], mkT[par][:, js], mqT[par][:, js], True, True),
                             (ps[4][:, 128:385], kk[par][:, j, :], vt[par][:, j, :], True, True, True)],
                            ["mkT%d" % par, "mqT%d" % par, "kk%d" % par, "vt%d" % par], ["b4"])
                        TT("dve", Sm[jp][:, :], ps[4][:, 0:128], maskST[:, :], ALU.mult, ["b4", "maskST"], ["Sm%d" % jp])
                        ACT(Cb[jp][:, :], X[:, :], AF.Copy, ["X", dk], ["Cb%d" % jp], scale=decbuf[par][:, j:j + 1])
                        STT(X[:, :], X[:, :], decbuf[par][:, j:j + 1], ps[4][:, 128:385], ALU.mult, ALU.add,
                            ["X", dk, "b4"], ["X"])
                        MMG([(pN[:, 0:257], mqT[par][:, js], Cb[jp][:, :], True, False),
                             (pN[:, 0:257], Sm[jp][:, :], vt[par][:, j, :], False, True)],
                            ["mqT%d" % par, "Cb%d" % jp, "Sm%d" % jp, "vt%d" % par], [bN])
                        TSC("dve", s1[:, 6:7], pN[:, 256:257], -1.0, gfl[par][:, j:j + 1], ALU.mult, ALU.max,
                            [bN, "gfl%d" % par], [sk + "den0"])
                        TT("dve", s1[:, 0:1], s1[:, 6:7], pN[:, 256:257], ALU.max, [sk + "den0", bN], [sk + "den"])
                        RECIP(s1[:, 1:2], s1[:, 0:1], [sk + "den"], [sk + "rden"])
                        ACT(junk[jp][:, :], pN[:, 0:256], AF.Square, [bN, sk + "rden"], ["junk%d" % jp, sk + "ss"],
                            scale=s1[:, 1:2], accum=s1[:, 2:3])
                        ACT(s1[:, 3:4], s1[:, 2:3], AF.Sqrt, [sk + "ss", "epsc"], [sk + "sq"], bias=epsc[:, 0:1],
                            scale=1.0 / 256.0)
                        RECIP(s1[:, 4:5], s1[:, 3:4], [sk + "sq"], [sk + "rstd"])
                        TT("dve", s1[:, 5:6], s1[:, 4:5], s1[:, 1:2], ALU.mult, [sk + "rstd", sk + "rden"], [sk + "sc"])
                        TT("pool", gg[jp][:, :], mg[:, :], og[par][:, j, :], ALU.mult, ["mg", "og%d" % par], ["gg%d" % jp])
                        STT(ybf[jp][:, :], pN[:, 0:128], s1[:, 5:6], gg[jp][:, :], ALU.mult, ALU.mult,
                            [bN, sk + "sc", "gg%d" % jp], ["ybf%d" % jp])
                        TRP(psT[:, 384:512], ybf[jp][:, :], ident[:, :], ["ybf%d" % jp, "ident"], ["b7"])
                        COPY("act", zTm[:, js], psT[:, 384:512], ["b7"], ["zTm"])
                    DMA("sp", agin[128:256, tsl], zTm[:, :], "zoutm", ["zTm"], ["aginm"])

                for t in range(-2, n_tiles_a):
                    lists = []
                    for st, gg_ in ((stage_F, t + 2), (stage_P, t + 1), (stage_M, t)):
                        if 0 <= gg_ < n_tiles_a and not (st is stage_M and 'mlstm' in skip):
                            P.begin()
                            st(gg_)
                            lists.append(P.end())
                    P.replay(lists)
                P.barrier()
                P.emit()

            with ExitStack() as e2:
                Pb = [[sb(e2, f"P{s_}c{c}", [128, TS_], BF16) for c in range(2)] for s_ in range(3)]
                accD = [sb(e2, f"accD{c}", [128, TS_], F32) for c in range(2)]
                rb1 = sb(e2, "rb1", [128, TS_], F32)
                rb2 = sb(e2, "rb2", [128, TS_], F32)
                tA = sb(e2, "tA", [128, TS_], F32)
                QT2 = [sb(e2, f"QT2_{i}", [128, TS_], BF16) for i in range(2)]
                zTa = sb(e2, "zTa", [128, TS_], BF16)
                KT = sb(e2, "KT", [128, S], BF16)
                Vaug = sb(e2, "Vaug", [128, S // 128, 128], BF16)
                Vflat = Vaug[:, :, :].rearrange("p a b -> p (a b)")
                for ch in range((S_A + 2047) // 2048):
                    cs = slice(ch * 2048, min((ch + 1) * 2048, S_A))
                    DMA("sp", KT[:, cs], kT_d[:, cs], "kvin", [], ["KVk%d" % ch])
                    DMA("sp", Vflat[:, cs], v_d[:, cs], "kvin", [], ["KVv%d" % ch])
                P.bw["KV"] = ("kvin", P.cnt["kvin"])

                for g in (range(n_tiles_a) if 'attn' not in skip else ()):
                    par = g % 2
                    tsl = slice(g * TS_, (g + 1) * TS_)
                    qkey = "QT%d" % par
                    QTg = QT2[par]
                    DMA("sp", QTg[:, :], qT_d[:, tsl], "qtin%d" % par, [], [qkey])
                    nkb = 4 * g + 4

                    def S_stage(kb):
                        r = kb - 4 * g
                        q0 = 128 * max(r, 0)
                        n = TS_ - q0
                        s = kb % 2
                        p3 = kb % 3
                        kcols = slice(kb * 128, kb * 128 + 128)
                        for comp in range(2):
                            prt = slice(64 * comp, 64 * comp + 64)
                            bank = ps[2 * s + comp]
                            bkey = "b%d" % (2 * s + comp)
                            MMG([(bank[:, 0:n], KT[prt, kcols], QTg[prt, q0:TS_], True, True)], [qkey, "KV"], [bkey])
                            pkey = "P%dc%d" % (p3, comp)
                            ACT(Pb[p3][comp][:, 0:n], bank[:, 0:n], AF.Exp, [bkey], [pkey], scale=0.125)
                            if r >= 0:
                                TT("pool", Pb[p3][comp][:, 0:128], Pb[p3][comp][:, 0:128], maskA[:, :], ALU.mult,
                                   [pkey, "maskA"], [pkey])

                    def PV_stage(kb):
                        r = kb - 4 * g
                        q0 = 128 * max(r, 0)
                        n = TS_ - q0
                        p3 = kb % 3
                        for comp in range(2):
                            pkey = "P%dc%d" % (p3, comp)
                            MMG([(ps[4 + comp][:, q0:TS_], Vaug[:, kb, :], Pb[p3][comp][:, 0:n], kb == 0, kb == nkb - 1)],
                                [pkey, "KV"], ["b%d" % (4 + comp)])
                            eng = "dve" if comp == 0 else "pool"
                            akey = "accD%d" % comp
                            if kb == 0:
                                COPY(eng, accD[comp][:, :], Pb[p3][comp][:, :], [pkey], [akey])
                            else:
                                TT(eng, accD[comp][:, q0:TS_], accD[comp][:, q0:TS_], Pb[p3][comp][:, 0:n], ALU.add,
                                   [pkey, akey], [akey])

                    S_stage(0)
                    for kb in range(nkb):
                        if kb + 1 < nkb:
                            S_stage(kb + 1)
                        PV_stage(kb)
                    MMG([(ps[6][:, :], ones_f[:, :], accD[0][:, :], True, True)], ["ones_f", "accD0"], ["b6"])
                    RECIP(rb1[:, :], ps[6][:, :], ["b6"], ["rb1"])
                    MMG([(ps[6][:, :], ones_f[:, :], accD[1][:, :], True, True)], ["ones_f", "accD1"], ["b6"])
                    RECIP(rb2[:, :], ps[6][:, :], ["b6"], ["rb2"])
                    TT("dve", tA[:, :], ps[4][:, :], rb1[:, :], ALU.mult, ["b4", "rb1"], ["tA"])
                    STT(rb1[:, :], ps[5][:, :], lams[:, 3:4], rb2[:, :], ALU.mult, ALU.mult, ["b5", "nlam", "rb2", "rb1"], ["rb1"])
                    TT("pool", tA[:, :], tA[:, :], rb1[:, :], ALU.add, ["tA", "rb1"], ["tA"])
                    ACT(rb2[:, :], tA[:, :], AF.Square, ["tA"], ["rb2"])
                    MMG([(ps[6][:, :], ones_f[:, :], rb2[:, :], True, True)], ["ones_f", "rb2"], ["b6"])
                    ACT(rb1[:, :], ps[6][:, :], AF.Sqrt, ["b6", "epsc"], ["rb1"], bias=epsc[:, 0:1], scale=1.0 / 128.0)
                    RECIP(rb1[:, :], rb1[:, :], ["rb1"], ["rb1"])
                    STT(zTa[:, :], tA[:, :], sgcol[:, 0:1], rb1[:, :], ALU.mult, ALU.mult, ["tA", "sgcol", "rb1"], ["zTa"])
                    DMA("sp", agin[0:128, tsl], zTa[:, :], "zouta", ["zTa"], ["agina"])
                P.barrier()
                P.emit()

        if do_phase_b:
            P.dma("pool", lambda e: e.collective_compute("AllGather", ALU.bypass,
                                                         replica_groups=[list(range(NCORE))],
                                                         ins=[agin[:, :]], outs=[agout[:, :]]),
                  "cc", [], ["agout"], inc=1)
            with ExitStack() as eb:
                zt = sb(eb, "zt", [128, 16, 1024], BF16)
                x1 = sb(eb, "x1", [128, 16, 1024], F32)
                h2 = sb(eb, "h2", [128, 16, 1024], BF16)
                wgt = [sb(eb, f"wgt{i}", [128, 16, FG], BF16) for i in range(2)]
                wut = [sb(eb, f"wut{i}", [128, 16, FG], BF16) for i in range(2)]
                wdt = [sb(eb, f"wdt{i}", [128, 2, D], BF16) for i in range(2)]
                AT = sb(eb, "AT", [128, 2, 1024], BF16)
                sgb = sb(eb, "sgb", [128, TS_], F32)
                tgb = sb(eb, "tgb", [128, TS_], F32)
                rtmpB = sb(eb, "rtmpB", [128, TS_], F32)
                rstdB = sb(eb, "rstdB", [128, TS_], F32)
                g2 = sb(eb, "g2s", [128, 16], F32)
                gf = sb(eb, "gfs", [128, 16], F32)
                DMA("sp", g2[:, :], g2_d[:, :], "cstb", [], ["g2"])
                DMA("sp", gf[:, :], gf_d[:, :], "cstb", [], ["gf"])
                cstb_final = ("cstb", P.cnt["cstb"])
                P.bw["g2"] = cstb_final
                P.bw["gf"] = cstb_final

                ag_v = agout.rearrange("(c p) t -> p c t", p=128)
                xTb_v = xTb.rearrange("(c p) t -> p c t", p=128)
                outT_v = outT.rearrange("(c p) t -> p c t", p=128)
                wo_v = wo_d.rearrange("(c p) n -> p c n", p=128)
                wg_v = wg_d.rearrange("(c p) n -> p c n", p=128)
                wu_v = wu_d.rearrange("(c p) n -> p c n", p=128)
                wd_v = wd_d.rearrange("(c p) n -> p c n", p=128)

                def rstdB_compute(tile):
                    tl = slice(tile * TS_, (tile + 1) * TS_)
                    sq3 = zt[:, :, 0:TS_]
                    ACT(sq3, x1[:, :, tl], AF.Square, ["x1"], ["zt"])
                    MMG([(ps[0][:, :], ones_bf[:, :], sq3[:, c, :], c == 0, c == 15) for c in range(16)],
                        ["zt", "ones_bf"], ["b0"])
                    ACT(rtmpB[:, :], ps[0][:, :], AF.Sqrt, ["b0", "epsc"], ["rtmpB"], bias=epsc[:, 0:1], scale=1.0 / D)
                    RECIP(rstdB[:, :], rtmpB[:, :], ["rtmpB"], ["rstdB"])

                wcount = [0]

                for pss in range(2):
                    def ztload(e, pss=pss):
                        pid = e.partition_id()
                        return e.dma_start(out=zt[:, :, :], in_=ag_v[:, :, bass.ds(pid * TOKB + pss * 1024, 1024)])
                    P.dma("sp", ztload, "ztl", ["agout"], ["zt"])
                    DMA("sp", x1[:, :, :], xTb_v[:, :, pss * 1024:(pss + 1) * 1024], "x1l", [], ["x1"])
                    for i in range(8):
                        wb = wgt[i % 2]
                        wkey = "wgt%d" % (i % 2)
                        DMA("pool", wb[:, :, :], wo_v[:, :, FG * i:FG * (i + 1)], wkey + "l", [], [wkey])
                        for dc in range(2):
                            dch = 2 * i + dc
                            for tile in range(2):
                                tl = slice(tile * TS_, (tile + 1) * TS_)
                                bank = ps[1 + (dc * 2 + tile) % 4]
                                bkey = "b%d" % (1 + (dc * 2 + tile) % 4)
                                MMG([(bank[:, :], wb[:, fc, 128 * dc:128 * dc + 128], zt[:, fc, tl], fc == 0, fc == 15)
                                     for fc in range(16)], [wkey, "zt"], [bkey])
                                TT("dve", x1[:, dch, tl], x1[:, dch, tl], bank[:, :], ALU.add, ["x1", bkey], ["x1"])
                    for tile in range(2):
                        tl = slice(tile * TS_, (tile + 1) * TS_)
                        rstdB_compute(tile)
                        for c in range(16):
                            STT(h2[:, c, tl], x1[:, c, tl], g2[:, c:c + 1], rstdB[:, :], ALU.mult, ALU.mult,
                                ["x1", "g2", "rstdB"], ["h2"])
                    for fg in range(NFG):
                        sl = wcount[0] % 2
                        wcount[0] += 1
                        fcols = slice(fg * FG, (fg + 1) * FG)
                        DMA("pool", wgt[sl][:, :, :], wg_v[:, :, fcols], "wgt%dl" % sl, [], ["wgt%d" % sl])
                        DMA("pool", wut[sl][:, :, :], wu_v[:, :, fcols], "wut%dl" % sl, [], ["wut%d" % sl])
                        DMA("pool", wdt[sl][:, :, :], wd_v[:, 2 * fg:2 * fg + 2, :], "wdt%dl" % sl, [], ["wdt%d" % sl])
                        for fc in range(2):
                            for tile in range(2):
                                tl = slice(tile * TS_, (tile + 1) * TS_)
                                bg_ = 1 + 2 * ((fc * 2 + tile) % 2)
                                bu_ = bg_ + 1
                                MMG([(ps[bg_][:, :], wgt[sl][:, c, 128 * fc:128 * fc + 128], h2[:, c, tl], c == 0, c == 15)
                                     for c in range(16)], ["wgt%d" % sl, "h2"], ["b%d" % bg_])
                                MMG([(ps[bu_][:, :], wut[sl][:, c, 128 * fc:128 * fc + 128], h2[:, c, tl], c == 0, c == 15)
                                     for c in range(16)], ["wut%d" % sl, "h2"], ["b%d" % bu_])
                                ACT(sgb[:, :], ps[bg_][:, :], AF.Sigmoid, ["b%d" % bg_], ["sgb"])
                                TT("dve", tgb[:, :], ps[bg_][:, :], sgb[:, :], ALU.mult, ["b%d" % bg_, "sgb"], ["tgb"])
                                TT("dve", AT[:, fc, tl], tgb[:, :], ps[bu_][:, :], ALU.mult, ["tgb", "b%d" % bu_], ["AT"])
                        for dc in range(16):
                            for tile in range(2):
                                tl = slice(tile * TS_, (tile + 1) * TS_)
                                bi = 5 + (dc * 2 + tile) % 2
                                bank = ps[bi]
                                bkey = "b%d" % bi
                                MMG([(bank[:, :], wdt[sl][:, fc, 128 * dc:128 * dc + 128], AT[:, fc, tl], fc == 0, fc == 1)
                                     for fc in range(2)], ["wdt%d" % sl, "AT"], [bkey])
                                TT("dve", x1[:, dc, tl], x1[:, dc, tl], bank[:, :], ALU.add, ["x1", bkey], ["x1"])
                    for tile in range(2):
                        tl = slice(tile * TS_, (tile + 1) * TS_)
                        rstdB_compute(tile)
                        for c in range(16):
                            STT(x1[:, c, tl], x1[:, c, tl], gf[:, c:c + 1], rstdB[:, :], ALU.mult, ALU.mult,
                                ["x1", "gf", "rstdB"], ["x1"])
                    DMA("sp", outT_v[:, :, pss * 1024:(pss + 1) * 1024], x1[:, :, :], "outl", ["x1"], ["outT"])
                P.barrier()
                P.emit()
    return nc


def _host_constants():
    half = 32
    inv = (10000.0 ** (-np.arange(half, dtype=np.float32) / np.float32(half))).astype(np.float32)
    pos = np.arange(S, dtype=np.float32)
    ang = (pos[:, None] * inv[None, :]).astype(np.float32)
    cost = np.cos(ang).astype(np.float32)
    sint = np.sin(ang).astype(np.float32)
    ident = np.eye(128, dtype=np.float32)
    s_idx = np.arange(128)[:, None]
    t_idx = np.arange(128)[None, :]
    t2 = (s_idx <= t_idx).astype(np.float32)
    sel = np.zeros((128, 128), np.float32)
    sel[127, :] = 1.0
    maskst = (s_idx <= t_idx).astype(np.float32)
    maska = ((s_idx // 64) <= (t_idx // 64)).astype(np.float32)
    return dict(cost=cost, sint=sint, ident=ident, t2=t2, sel127=sel, maskst=maskst, maska=maska)


def _prep_inputs(inputs):
    f = lambda a: np.ascontiguousarray(np.asarray(a, dtype=np.float32))
    x = f(inputs["x"])[0]
    xT = np.ascontiguousarray(x.T)
    w_in = f(inputs["w_in"])[0]
    conv_w = f(inputs["conv_w"])[0]
    conv_b = f(inputs["conv_b"])[0]
    b_i = f(inputs["b_igate"])[0]
    b_f = f(inputs["b_fgate"])[0]
    mnorm = f(inputs["mnorm_g"])[0]
    w_out = f(inputs["w_out"])[0]
    consts = _host_constants()

    def col16(v):
        return np.ascontiguousarray(v.reshape(16, 128).T)

    lamv = np.stack([f(inputs["lambda_q1"])[0], f(inputs["lambda_q2"])[0],
                     f(inputs["lambda_k1"])[0], f(inputs["lambda_k2"])[0]], axis=0)
    lamv = np.ascontiguousarray(np.broadcast_to(lamv[None], (128, 4, 64)))
    sg = np.ascontiguousarray(f(inputs["subln_g"])[0].reshape(128, 1))

    wo = np.empty_like(w_out)
    for s in range(NCORE):
        wo[(2 * s) * 128:(2 * s + 1) * 128] = w_out[1024 + 128 * s:1024 + 128 * (s + 1)]
        wo[(2 * s + 1) * 128:(2 * s + 2) * 128] = w_out[128 * s:128 * (s + 1)]

    shared = dict(
        xT=xT, g1=col16(f(inputs["norm1_g"])[0]), lamv=lamv, sg=sg,
        cost=consts["cost"], sint=consts["sint"], ident=consts["ident"], t2=consts["t2"],
        sel127=consts["sel127"], maskst=consts["maskst"], maska=consts["maska"],
        wo=wo, g2=col16(f(inputs["norm2_g"])[0]), wgate=f(inputs["w_gate"])[0], wup=f(inputs["w_up"])[0],
        wdown=f(inputs["w_down"])[0], gf=col16(f(inputs["final_g"])),
    )
    in_maps = []
    for c in range(NCORE):
        a = c
        hm = c // 2
        hf = c % 2
        aq = w_in[:, 3080 + a * 128:3080 + (a + 1) * 128]
        ak = w_in[:, 4104 + a * 128:4104 + (a + 1) * 128]
        av = w_in[:, 5128 + a * 128:5128 + (a + 1) * 128]
        mv0 = w_in[:, 1024 + hm * 256 + hf * 128:1024 + hm * 256 + hf * 128 + 128]
        mv1 = w_in[:, 1024 + hm * 256 + (1 - hf) * 128:1024 + hm * 256 + (1 - hf) * 128 + 128]
        mo = w_in[:, 2048 + hm * 256 + hf * 128:2048 + hm * 256 + hf * 128 + 128]
        mi = w_in[:, 3072 + hm:3073 + hm]
        mf = w_in[:, 3076 + hm:3077 + hm]
        wtok = np.ascontiguousarray(np.concatenate([aq, ak, av, mv0, mv1, mo, mi, mf], axis=1))
        mq = w_in[:, hm * 128:(hm + 1) * 128]
        mk = w_in[:, 512 + hm * 128:512 + (hm + 1) * 128]
        wfm = np.ascontiguousarray(np.concatenate([mq, mk], axis=1))
        convw = np.ascontiguousarray(np.concatenate(
            [conv_w[:, hm * 128:(hm + 1) * 128].T, conv_w[:, 512 + hm * 128:512 + (hm + 1) * 128].T], axis=1))
        convb = np.ascontiguousarray(np.stack(
            [conv_b[hm * 128:(hm + 1) * 128], conv_b[512 + hm * 128:512 + (hm + 1) * 128]], axis=1))
        bgv = np.ascontiguousarray(np.broadcast_to(np.array([b_i[hm], b_f[hm]], np.float32)[None, :], (128, 2)))
        mgv = np.ascontiguousarray(np.broadcast_to(mnorm[hm, hf * 128:(hf + 1) * 128][None, :], (128, 128)))
        m = dict(shared)
        m.update(xTb=np.ascontiguousarray(xT[:, c * TOKB:(c + 1) * TOKB]), wtok=wtok, wf=wfm,
                 convw=convw, convb=convb, bg=bgv, mg=mgv)
        in_maps.append(m)
    return in_maps


def kernel(**inputs):
    in_maps = _prep_inputs(inputs)
    nc = build_program()
    res = run_bass_kernel_spmd(nc, in_maps, core_ids=list(range(NCORE)))
    out = np.empty((1, S, D), np.float32)
    for c in range(NCORE):
        oT = np.asarray(res.results[c]["outT"], dtype=np.float32)
        out[0, c * TOKB:(c + 1) * TOKB, :] = oT.T
    return out
```
